# Optimizing a Trainium2 kernel written in Bass

```python
import jax, jax.numpy as jnp
from jax import lax
import numpy as np

D_MODEL = 1024
BATCH = 32
SEQ = 2048
DEPTH = 1

GRID_W = 64
CTX_LEN = 256
MLA_HEADS = 8
QK_NOPE = 64
QK_ROPE = 32
QK_HEAD = QK_NOPE + QK_ROPE
V_HEAD = 64
Q_LORA = 256
KV_LORA = 128
AXIS_DIM = QK_ROPE // 2
ROPE_BASE = 10000.0
Q_BLOCK = 128
MLA_WIDTH = MLA_HEADS * V_HEAD
GMLP_GROUPS = 8
GMLP_GROUP_DIM = 64
GMLP_WIDTH = GMLP_GROUPS * GMLP_GROUP_DIM
CHUNK = 128
D_MIX = MLA_WIDTH + GMLP_WIDTH
KV_COLS = KV_LORA + QK_ROPE
Q_START = KV_COLS
U_START = KV_COLS + Q_LORA
V_START = U_START + GMLP_WIDTH
IN_COLS = V_START + GMLP_WIDTH
D_FF = 2816
N_MOD = 9
EPS = 1e-6

kernel_name = "hymba_mla_gmlp_macaron_dit_layer"


def rms_norm(x, w):
    xf = x.astype(jnp.float32)
    y = xf * lax.rsqrt(jnp.mean(xf * xf, axis=-1, keepdims=True) + EPS)
    return (y * w.astype(jnp.float32)).astype(x.dtype)


def modulate(h, shift, scale):
    return h * (1 + scale) + shift


def swiglu(h, w1, w3, w2):
    return (jax.nn.silu(h @ w1) * (h @ w3)) @ w2


def ffn_sublayer(h_in, shift, scale, gate, norm_w, w1, w3, w2):
    h = modulate(rms_norm(h_in, norm_w), shift, scale)
    return h_in + 0.5 * gate * swiglu(h, w1, w3, w2)


def axial_rope(x, cos, sin):
    xr = x.reshape(x.shape[:-1] + (2, 2, AXIS_DIM // 2))
    rot = jnp.stack([-xr[..., 1, :], xr[..., 0, :]], axis=-2).reshape(x.shape)
    return x * cos[:, None, :] + rot * sin[:, None, :]


def rope_part(x, rope):
    if rope is None:
        return x
    return jnp.concatenate([x[..., :QK_NOPE], axial_rope(x[..., QK_NOPE:], *rope)], axis=-1)


def mla_keys_values(kv_proj, kv_a_norm_w, w_ukv, k_norm_w, rope):
    B, S, _ = kv_proj.shape
    c_kv = rms_norm(kv_proj[..., :KV_LORA], kv_a_norm_w)
    k_pe = kv_proj[..., KV_LORA:]
    kv = (c_kv @ w_ukv).reshape(B, S, MLA_HEADS, QK_NOPE + V_HEAD)
    k_nope, v = kv[..., :QK_NOPE], kv[..., QK_NOPE:]
    k_pe = jnp.broadcast_to(k_pe[:, :, None, :], (B, S, MLA_HEADS, QK_ROPE))
    k = rms_norm(jnp.concatenate([k_nope, k_pe], axis=-1), k_norm_w)
    return rope_part(k, rope), v


def mla_queries(q_proj, q_a_norm_w, w_uq, q_norm_w, rope):
    B, S, _ = q_proj.shape
    c_q = rms_norm(q_proj, q_a_norm_w)
    q = (c_q @ w_uq).reshape(B, S, MLA_HEADS, QK_HEAD)
    return rope_part(rms_norm(q, q_norm_w), rope)


def block_attention(q, k_all, v_all):
    B, S, H, Dk = q.shape
    nb = S // Q_BLOCK
    scale = Dk ** -0.5
    qb = jnp.moveaxis(q.reshape(B, nb, Q_BLOCK, H, Dk), 1, 0)

    def one_block(q_blk):
        s = jnp.einsum('bqhd,bkhd->bhqk', q_blk, k_all).astype(jnp.float32) * scale
        p = jax.nn.softmax(s, axis=-1).astype(v_all.dtype)
        return jnp.einsum('bhqk,bkhd->bqhd', p, v_all)

    out = lax.map(one_block, qb)
    return jnp.moveaxis(out, 0, 1).reshape(B, S, H * V_HEAD)


def chunk_gmlp(u, v, v_norm_w, w_s, b_s):
    B, S, _ = u.shape
    n = S // CHUNK
    u = jax.nn.gelu(u).reshape(B, n, CHUNK, GMLP_GROUPS, GMLP_GROUP_DIM)
    v = rms_norm(jax.nn.gelu(v).reshape(B, n, CHUNK, GMLP_GROUPS, GMLP_GROUP_DIM), v_norm_w)
    s = jnp.einsum('gpq,bnqgc->bnpgc', w_s, v) + b_s.T[:, :, None]
    return (u * s).reshape(B, S, GMLP_WIDTH)


def token_mix(proj, k_all, v_all, rope, q_a_norm_w, w_uq, q_norm_w, v_norm_w, w_s, b_s, w_out):
    q = mla_queries(proj[..., Q_START:U_START], q_a_norm_w, w_uq, q_norm_w, rope)
    attn = block_attention(q, k_all, v_all)
    sg = chunk_gmlp(proj[..., U_START:V_START], proj[..., V_START:], v_norm_w, w_s, b_s)
    return jnp.concatenate([attn, sg], axis=-1) @ w_out


def hybrid_layer(x, ctx, c, c_ctx, cos, sin,
                 w_ada, b_ada, norm1_w, ffn1_w1, ffn1_w3, ffn1_w2,
                 norm2_w, w_in, q_a_norm_w, w_uq, kv_a_norm_w, w_ukv, q_norm_w, k_norm_w,
                 v_norm_w, w_s, b_s, w_out,
                 norm3_w, ffn2_w1, ffn2_w3, ffn2_w2, update_ctx):
    mx = jnp.split((jax.nn.silu(c) @ w_ada + b_ada)[:, None, :], N_MOD, axis=-1)
    mc = jnp.split((jax.nn.silu(c_ctx) @ w_ada + b_ada)[None, None, :], N_MOD, axis=-1)
    rope = (cos, sin)

    x = ffn_sublayer(x, mx[0], mx[1], mx[2], norm1_w, ffn1_w1, ffn1_w3, ffn1_w2)
    ctx = ffn_sublayer(ctx, mc[0], mc[1], mc[2], norm1_w, ffn1_w1, ffn1_w3, ffn1_w2)

    proj = modulate(rms_norm(x, norm2_w), mx[3], mx[4]) @ w_in
    hc = modulate(rms_norm(ctx, norm2_w), mc[3], mc[4])
    proj_c = hc @ (w_in if update_ctx else w_in[:, :KV_COLS])
    k_lat, v_lat = mla_keys_values(proj[..., :KV_COLS], kv_a_norm_w, w_ukv, k_norm_w, rope)
    k_ctx, v_ctx = mla_keys_values(proj_c[..., :KV_COLS], kv_a_norm_w, w_ukv, k_norm_w, None)
    k_all = jnp.concatenate([k_lat, k_ctx], axis=1)
    v_all = jnp.concatenate([v_lat, v_ctx], axis=1)
    x = x + mx[5] * token_mix(proj, k_all, v_all, rope, q_a_norm_w, w_uq, q_norm_w,
                              v_norm_w, w_s, b_s, w_out)
    if update_ctx:
        ctx = ctx + mc[5] * token_mix(proj_c, k_ctx, v_ctx, None, q_a_norm_w, w_uq, q_norm_w,
                                      v_norm_w, w_s, b_s, w_out)
        ctx = ffn_sublayer(ctx, mc[6], mc[7], mc[8], norm3_w, ffn2_w1, ffn2_w3, ffn2_w2)

    x = ffn_sublayer(x, mx[6], mx[7], mx[8], norm3_w, ffn2_w1, ffn2_w3, ffn2_w2)
    return x, ctx


def setup_inputs(seed: int = 0) -> dict:
    key = jax.random.key(seed)
    ks = jax.random.split(key, 26)
    f32 = jnp.float32

    def dense(k, shape, fan_in, gain=1.0):
        return jax.random.normal(k, shape, f32) * (gain * fan_in ** -0.5)

    def gain_vec(k, shape):
        return 1.0 + 0.02 * jax.random.normal(k, shape, f32)

    L = DEPTH
    return {
        "x": jax.random.normal(ks[0], (BATCH, SEQ, D_MODEL), f32),
        "c": jax.random.normal(ks[1], (BATCH, D_MODEL), f32),
        "ctx": jax.random.normal(ks[2], (BATCH, CTX_LEN, D_MODEL), f32),
        "c_ctx": jax.random.normal(ks[3], (D_MODEL,), f32),
        "w_ada": dense(ks[4], (L, D_MODEL, N_MOD * D_MODEL), D_MODEL, 0.5),
        "b_ada": 0.02 * jax.random.normal(ks[5], (L, N_MOD * D_MODEL), f32),
        "norm1_w": gain_vec(ks[6], (L, D_MODEL)),
        "ffn1_w1": dense(ks[7], (L, D_MODEL, D_FF), D_MODEL),
        "ffn1_w3": dense(ks[8], (L, D_MODEL, D_FF), D_MODEL),
        "ffn1_w2": dense(ks[9], (L, D_FF, D_MODEL), D_FF),
        "norm2_w": gain_vec(ks[10], (L, D_MODEL)),
        "w_in": dense(ks[11], (L, D_MODEL, IN_COLS), D_MODEL),
        "q_a_norm_w": gain_vec(ks[12], (L, Q_LORA)),
        "w_uq": dense(ks[13], (L, Q_LORA, MLA_HEADS * QK_HEAD), Q_LORA),
        "kv_a_norm_w": gain_vec(ks[14], (L, KV_LORA)),
        "w_ukv": dense(ks[15], (L, KV_LORA, MLA_HEADS * (QK_NOPE + V_HEAD)), KV_LORA),
        "q_norm_w": gain_vec(ks[16], (L, QK_HEAD)),
        "k_norm_w": gain_vec(ks[17], (L, QK_HEAD)),
        "v_norm_w": gain_vec(ks[18], (L, GMLP_GROUPS, GMLP_GROUP_DIM)),
        "w_s": dense(ks[19], (L, GMLP_GROUPS, CHUNK, CHUNK), CHUNK),
        "b_s": gain_vec(ks[20], (L, GMLP_GROUPS, CHUNK)),
        "w_out": dense(ks[21], (L, D_MIX, D_MODEL), D_MIX),
        "norm3_w": gain_vec(ks[22], (L, D_MODEL)),
        "ffn2_w1": dense(ks[23], (L, D_MODEL, D_FF), D_MODEL),
        "ffn2_w3": dense(ks[24], (L, D_MODEL, D_FF), D_MODEL),
        "ffn2_w2": dense(ks[25], (L, D_FF, D_MODEL), D_FF),
    }


def reference(x, c, ctx, c_ctx, w_ada, b_ada, norm1_w, ffn1_w1, ffn1_w3, ffn1_w2,
              norm2_w, w_in, q_a_norm_w, w_uq, kv_a_norm_w, w_ukv, q_norm_w, k_norm_w,
              v_norm_w, w_s, b_s, w_out, norm3_w, ffn2_w1, ffn2_w3, ffn2_w2):
    S = x.shape[1]
    ROWS = S // GRID_W
    f32 = jnp.float32
    rows = jnp.repeat(jnp.arange(ROWS, dtype=f32), GRID_W)
    cols = jnp.tile(jnp.arange(GRID_W, dtype=f32), ROWS)
    inv = ROPE_BASE ** (-jnp.arange(0, AXIS_DIM, 2, dtype=f32) / AXIS_DIM)
    ang_r = rows[:, None] * inv
    ang_c = cols[:, None] * inv
    ang = jnp.concatenate([ang_r, ang_r, ang_c, ang_c], axis=-1)
    cos = jnp.cos(ang).astype(x.dtype)
    sin = jnp.sin(ang).astype(x.dtype)

    layer_weights = (w_ada, b_ada, norm1_w, ffn1_w1, ffn1_w3, ffn1_w2,
                     norm2_w, w_in, q_a_norm_w, w_uq, kv_a_norm_w, w_ukv, q_norm_w, k_norm_w,
                     v_norm_w, w_s, b_s, w_out, norm3_w, ffn2_w1, ffn2_w3, ffn2_w2)
    for i in range(DEPTH):
        x, ctx = hybrid_layer(x, ctx, c, c_ctx, cos, sin, *[w[i] for w in layer_weights],
                              update_ctx=i < DEPTH - 1)
    return x
```

```python
import numpy as np
import ml_dtypes
import concourse.bass as bass
import concourse.mybir as mybir
from concourse.bass_utils import run_bass_kernel_spmd

F32 = mybir.dt.float32
BF16 = mybir.dt.bfloat16
ALU = mybir.AluOpType
AF = mybir.ActivationFunctionType
AX = mybir.AxisListType

D = 1024
S = 2048
CL = 256
DFF = 2816
NF = DFF // 128
NCORES = 8
EPS = 1e-6
GRID_W = 64


class _Sem:
    def __init__(self, h):
        self.h = h
        self.cnt = 0


class _Eng:
    def __init__(self, ctx, e, name):
        self.e = e
        self.name = name
        self.sem = ctx.new_sem("c_" + name)
        self.waited = {}

    def wait(self, tok):
        sem, v = tok
        if sem is self.sem and self.name == "pe":
            return
        if self.waited.get(sem, 0) >= v:
            return
        self.e.wait_ge(sem.h, v)
        self.waited[sem] = v

    def signal(self, ins):
        self.sem.cnt += 1
        ins.then_inc(self.sem.h, 1)
        return (self.sem, self.sem.cnt)


class Ctx:
    def __init__(self, nc):
        self.nc = nc
        self.sems = []
        self.eng = {}
        for nm, e in (("pe", nc.tensor), ("act", nc.scalar), ("dve", nc.vector),
                      ("pool", nc.gpsimd), ("sp", nc.sync)):
            self.eng[nm] = _Eng(self, e, nm)
        self.last_w = {}
        self.readers = {}
        self.dma_sems = {}

    def new_sem(self, name):
        h = self.nc.semaphore(name).__enter__()
        s = _Sem(h)
        self.sems.append(s)
        return s

    def _deps(self, eng, reads, writes):
        for h in reads:
            t = self.last_w.get(h)
            if t is not None:
                eng.wait(t)
        for h in writes:
            t = self.last_w.get(h)
            if t is not None:
                eng.wait(t)
            for t in self.readers.get(h, ()):
                eng.wait(t)

    def _done(self, tok, reads, writes):
        for h in reads:
            self.readers.setdefault(h, []).append(tok)
        for h in writes:
            self.last_w[h] = tok
            self.readers[h] = []

    def op(self, engname, fn, reads=(), writes=()):
        eng = self.eng[engname]
        self._deps(eng, reads, writes)
        ins = fn(eng.e)
        tok = eng.signal(ins)
        self._done(tok, reads, writes)
        return tok

    def group(self, engname, fns, reads=(), writes=()):
        eng = self.eng[engname]
        self._deps(eng, reads, writes)
        ins = None
        for fn in fns:
            ins = fn(eng.e)
        tok = eng.signal(ins)
        self._done(tok, reads, writes)
        return tok

    def dma(self, qname, pairs, reads=(), writes=(), key=None):
        eng = self.eng[qname]
        self._deps(eng, reads, writes)
        if key is None:
            key = ("dma",) + tuple(writes) + tuple(reads)
        sem = self.dma_sems.get(key)
        if sem is None:
            sem = self.new_sem("d%d" % len(self.dma_sems))
            self.dma_sems[key] = sem
        for (o, i) in pairs:
            ins = eng.e.dma_start(out=o, in_=i)
            sem.cnt += 16
            ins.then_inc(sem.h, 16)
        tok = (sem, sem.cnt)
        self._done(tok, reads, writes)
        return tok

    def barrier(self):
        toks = [(e.sem, e.sem.cnt) for e in self.eng.values() if e.sem.cnt > 0]
        toks += [(s, s.cnt) for s in self.dma_sems.values() if s.cnt > 0]
        for e in self.eng.values():
            for t in toks:
                if t[0] is e.sem:
                    continue
                e.wait(t)
        self.last_w.clear()
        self.readers.clear()

    def final_wait(self):
        toks = [(e.sem, e.sem.cnt) for e in self.eng.values() if e.sem.cnt > 0]
        toks += [(s, s.cnt) for s in self.dma_sems.values() if s.cnt > 0]
        e = self.eng["sp"]
        for t in toks:
            e.wait(t)


class Pool:
    _uid = [0]

    def __init__(self, nc):
        self.nc = nc
        self.guards = []

    def _nm(self, name):
        Pool._uid[0] += 1
        return "%s_%d" % (name, Pool._uid[0])

    def sb(self, name, shape, dt):
        g = self.nc.sbuf_tensor(self._nm(name), list(shape), dt)
        t = g.__enter__()
        self.guards.append(g)
        return t

    def ps(self, name, shape, dt):
        g = self.nc.psum_tensor(self._nm(name), list(shape), dt)
        t = g.__enter__()
        self.guards.append(g)
        return t

    def release(self):
        for g in reversed(self.guards):
            g.__exit__(None, None, None)
        self.guards = []


_DBG = {}


def build(nb, stop_after=9):
    R = nb + 1
    nc = bass.Bass("TRN2", target_bir_lowering=False)
    dt_in = lambda name, shape, dt=F32: nc.dram_tensor(name, list(shape), dt, kind="ExternalInput").ap()
    x_d = dt_in("x", [nb, S, D])
    ctx_d = dt_in("ctx", [nb, CL, D])
    cT_d = dt_in("cT", [128, 8, R])
    w_ada_d = dt_in("w_ada", [D, 9 * D])
    b_ada_d = dt_in("b_ada", [1, 9 * D])
    normw_d = dt_in("normw", [3, D])
    fw_d = {}
    for i in (1, 2):
        fw_d[i] = (dt_in("ffn%d_w1" % i, [D, DFF]), dt_in("ffn%d_w3" % i, [D, DFF]), dt_in("ffn%d_w2" % i, [DFF, D]))
    w_in_d = dt_in("w_in", [D, 1440])
    qa_d = dt_in("q_a_norm_w", [1, 256])
    w_uq_d = dt_in("w_uq", [256, 768])
    kva_d = dt_in("kv_a_norm_w", [1, 128])
    w_ukv_d = dt_in("w_ukv", [128, 1024])
    qn_d = dt_in("q_norm_w", [1, 96])
    kn_d = dt_in("k_norm_w", [1, 96])
    vn_d = dt_in("v_norm_w", [1, 512])
    wsT_d = dt_in("w_sT", [128, 8, 128])
    bsT_d = dt_in("b_sT", [128, 8])
    w_out_d = dt_in("w_out", [D, D])
    cos_d = dt_in("cos_t", [128, 16, 32])
    ssin_d = dt_in("ssin_t", [128, 16, 32])
    ident_d = dt_in("ident", [128, 128], BF16)
    out_d = nc.dram_tensor("out", [nb, S, D], F32, kind="ExternalOutput").ap()
    modtab = nc.dram_tensor("modtab", [R, 9 * D], F32).ap()
    x1s = nc.dram_tensor("x1s", [nb, S + CL, D], F32).ap()
    x2s = nc.dram_tensor("x2s", [nb, S, D], F32).ap()
    qts = nc.dram_tensor("qts", [nb, 96, 8, S], BF16).ap()
    sgs = nc.dram_tensor("sgs", [nb, 128, 4, S], BF16).ap()

    c = Ctx(nc)
    _DBG['ctx'] = c
    glob = Pool(nc)
    ident = glob.sb("ident", [128, 128], BF16)
    epsb = glob.sb("epsb", [128, 1], F32)
    oneb = glob.sb("oneb", [128, 1], F32)
    c.dma("sp", [(ident[:], ident_d)], writes=["ident"])
    c.op("dve", lambda e: e.memset(epsb[:], EPS), writes=["epsb"])
    c.op("dve", lambda e: e.memset(oneb[:], 1.0), writes=["oneb"])

    def rstd_from_ss(ss_ap, out_ap, ln_ap, n, hs, hl, ho):
        c.op("act", lambda e: e.activation(out=ln_ap, in_=ss_ap, func=AF.Ln, scale=1.0 / n, bias=epsb[:, 0:1]),
             reads=[hs, "epsb"], writes=[hl])
        c.op("act", lambda e: e.activation(out=out_ap, in_=ln_ap, func=AF.Exp, scale=-0.5),
             reads=[hl], writes=[ho])

    def phase0():
        p = Pool(nc)
        cT = p.sb("cT", [128, 8, R], F32)
        t1 = p.sb("p0t1", [128, 8, R], F32)
        t2 = p.sb("p0t2", [128, 8, R], F32)
        sc = p.sb("sc", [128, 8, R], BF16)
        wa = [p.sb("wa%d" % i, [128, 8, 512], BF16) for i in range(2)]
        mods = p.sb("mods", [R, 9 * D], F32)
        bada = p.sb("bada", [R, 9 * D], F32)
        nwb = p.sb("nwb", [R, 3, D], F32)
        tab = p.sb("tab", [R, 9, D], F32)
        pm = p.ps("pm", [128, 2, 512], F32)
        c.dma("sp", [(cT[:], cT_d)], writes=["cT"])
        c.dma("sp", [(bada[:], b_ada_d[0:1, :].partition_broadcast(R))], writes=["bada"])
        c.dma("sp", [(nwb[:, i, :], normw_d[i:i + 1, :].partition_broadcast(R)) for i in range(3)], writes=["nwb"])
        c.op("act", lambda e: e.activation(out=t1[:], in_=cT[:], func=AF.Exp, scale=-1.0), reads=["cT"], writes=["t1"])
        c.op("act", lambda e: e.activation(out=t2[:], in_=t1[:], func=AF.Ln, scale=1.0, bias=oneb[:, 0:1]), reads=["t1", "oneb"], writes=["t2"])
        c.op("act", lambda e: e.activation(out=t1[:], in_=t2[:], func=AF.Exp, scale=-1.0), reads=["t2"], writes=["t1"])
        c.op("dve", lambda e: e.tensor_tensor(out=sc[:], in0=cT[:], in1=t1[:], op=ALU.mult), reads=["cT", "t1"], writes=["sc"])
        wv = w_ada_d.rearrange("(k p) n -> p k n", p=128)
        for g in range(18):
            s = g % 2
            c.dma("pool", [(wa[s][:], wv[:, :, g * 512:(g + 1) * 512])], writes=[("wa", s)])
            c.group("pe", [lambda e, k=k, s=s: e.matmul(out=pm[0:R, s, :], lhsT=sc[:, k, :], rhs=wa[s][:, k, :],
                                                        start=(k == 0), stop=(k == 7)) for k in range(8)],
                    reads=["sc", ("wa", s)], writes=[("pm", s)])
            c.op("dve", lambda e, s=s, g=g: e.tensor_tensor(out=mods[:, g * 512:(g + 1) * 512], in0=pm[0:R, s, :],
                                                            in1=bada[:, g * 512:(g + 1) * 512], op=ALU.add),
                 reads=[("pm", s), "bada"], writes=["mods"])
        for i in range(3):
            mv = lambda j: mods[:, (3 * i + j) * D:(3 * i + j + 1) * D]
            c.op("dve", lambda e, i=i, mv=mv: e.scalar_tensor_tensor(out=tab[:, 3 * i, :], in0=mv(1), scalar=1.0, in1=nwb[:, i, :],
                                                                     op0=ALU.add, op1=ALU.mult),
                 reads=["mods", "nwb"], writes=["tab"])
            c.op("dve", lambda e, i=i, mv=mv: e.tensor_copy(out=tab[:, 3 * i + 1, :], in_=mv(0)), reads=["mods"], writes=["tab"])
            gsc = 1.0 if i == 1 else 0.5
            c.op("dve", lambda e, i=i, mv=mv, gsc=gsc: e.tensor_scalar(out=tab[:, 3 * i + 2, :], in0=mv(2), scalar1=gsc, scalar2=None,
                                                                       op0=ALU.mult),
                 reads=["mods"], writes=["tab"])
        c.dma("sp", [(modtab.rearrange("r (a d) -> r a d", d=D), tab[:])], reads=["tab"], writes=["modtab"])
        c.barrier()
        p.release()

    def ffn_phase(tag, wd, mod_i, tiles):
        w1_d, w3_d, w2_d = wd
        p = Pool(nc)
        w1s = p.sb(tag + "w1", [128, 8, DFF], BF16)
        w3s = p.sb(tag + "w3", [128, 8, DFF], BF16)
        w2s = p.sb(tag + "w2", [128, NF, D], BF16)
        Ab = p.sb(tag + "Ab", [128, D], F32)
        Bb = p.sb(tag + "Bb", [128, D], F32)
        Gb = p.sb(tag + "Gb", [128, D], F32)
        xt = [p.sb(tag + "xt", [128, 4, D], F32)] * 2
        hb = [p.sb(tag + "hb%d" % i, [128, D], BF16) for i in range(2)]
        junk = p.sb(tag + "junk", [128, D], BF16)
        tmpf = p.sb(tag + "tmpf", [128, D], F32)
        hT = [p.sb(tag + "hT", [128, 8, 512], BF16)] * 2
        gT = p.sb(tag + "gT", [128, NF, 512], BF16)
        sl = [p.sb(tag + "sl%d" % i, [128, 512], F32) for i in range(2)]
        ss = p.sb(tag + "ss", [128, 4], F32)
        lnv = p.sb(tag + "lnv", [128, 4], F32)
        rstd = p.sb(tag + "rstd", [128, 4], F32)
        psm = p.ps(tag + "psm", [128, 7, 512], F32)
        tpb = p.ps(tag + "tpb", [128, 1024], BF16)
        tps = [psm[:, 6, :].bitcast(BF16), tpb[:]]

        w1v = w1_d.rearrange("(k p) n -> p k n", p=128)
        w3v = w3_d.rearrange("(k p) n -> p k n", p=128)
        w2v = w2_d.rearrange("(f p) n -> p f n", p=128)
        c.dma("pool", [(w1s[:, k, :], w1v[:, k, :]) for k in range(8)], writes=["w1s"])
        c.dma("pool", [(w3s[:, k, :], w3v[:, k, :]) for k in range(8)], writes=["w3s"])
        c.dma("pool", [(w2s[:, f, :], w2v[:, f, :]) for f in range(NF)], writes=["w2s"])

        cur_row = [None]
        for ti, (src, dst, nsub, mrow) in enumerate(tiles):
            ntok = nsub * 128
            s = ti % 2
            if cur_row[0] != mrow:
                cur_row[0] = mrow
                for (buf, nm, idx) in ((Ab, "Ab", mod_i[0]), (Bb, "Bb", mod_i[1]), (Gb, "Gb", mod_i[2])):
                    c.dma("sp", [(buf[:], modtab[mrow:mrow + 1, idx * D:(idx + 1) * D].partition_broadcast(128))],
                          reads=["modtab"], writes=[nm])
            X = xt[s]
            hx = "xt"
            c.dma("sp", [(X[:, 0:nsub, :], src.rearrange("(j p) d -> p j d", p=128))], writes=[hx])
            c.op("dve", lambda e: e.memset(ss[:], 0.0), writes=["ss"])
            for j in range(nsub):
                c.op("act", lambda e, j=j: e.activation(out=junk[:], in_=X[:, j, :], func=AF.Square, accum_out=ss[:, j:j + 1]),
                     reads=[hx, "ss"], writes=["junk", "ss"])
            rstd_from_ss(ss[:, 0:nsub], rstd[:, 0:nsub], lnv[:, 0:nsub], D, "ss", "lnv", "rstd")
            for j in range(nsub):
                tf = tmpf
                c.op("dve", lambda e, j=j, tf=tf: e.scalar_tensor_tensor(out=tf[:], in0=X[:, j, :], scalar=rstd[:, j:j + 1], in1=Ab[:],
                                                                         op0=ALU.mult, op1=ALU.mult),
                     reads=[hx, "rstd", "Ab"], writes=["tmpf"])
                c.op("dve", lambda e, j=j, tf=tf: e.tensor_tensor(out=hb[j % 2][:], in0=tf[:], in1=Bb[:], op=ALU.add),
                     reads=["tmpf", "Bb"], writes=[("hb", j % 2)])
                tp = tps[j % 2]
                c.group("pe", [lambda e, j=j, k=k, tp=tp: e.transpose(out=tp[:, k * 128:(k + 1) * 128], in_=hb[j % 2][:, k * 128:(k + 1) * 128],
                                                                      identity=ident[:]) for k in range(8)],
                        reads=[("hb", j % 2), "ident"], writes=[("tp", j % 2)])
                c.op("act", lambda e, j=j, tp=tp: e.copy(out=hT[s][:, :, j * 128:(j + 1) * 128],
                                                         in_=tp.rearrange("p (k t) -> p k t", t=128)),
                     reads=[("tp", j % 2)], writes=[("hT", j)])
            hTr = [("hT", j) for j in range(nsub)]
            for f in range(NF):
                sa = f % 2
                for (wi, ws, hw) in ((0, w1s, "w1s"), (1, w3s, "w3s")):
                    c.group("pe", [lambda e, k=k, ws=ws, wi=wi: e.matmul(out=psm[:, 2 * sa + wi, 0:ntok], lhsT=ws[:, k, f * 128:(f + 1) * 128],
                                                                         rhs=hT[s][:, k, 0:ntok], start=(k == 0), stop=(k == 7))
                                   for k in range(8)],
                            reads=hTr + [hw], writes=[("ab", sa, wi)])
                c.op("act", lambda e, sa=sa: e.activation(out=sl[sa][:, 0:ntok], in_=psm[:, 2 * sa, 0:ntok], func=AF.Silu),
                     reads=[("ab", sa, 0)], writes=[("sl", sa)])
                c.op("dve", lambda e, sa=sa, f=f: e.tensor_tensor(out=gT[:, f, 0:ntok], in0=sl[sa][:, 0:ntok], in1=psm[:, 2 * sa + 1, 0:ntok],
                                                                  op=ALU.mult),
                     reads=[("sl", sa), ("ab", sa, 1)], writes=[("gT", f)])
            gTr = [("gT", f) for f in range(NF)]
            for j in range(nsub):
                for n in range(2):
                    ys = (2 * j + n) % 2
                    c.group("pe", [lambda e, f=f, j=j, n=n, ys=ys: e.matmul(out=psm[:, 4 + ys, :], lhsT=gT[:, f, j * 128:(j + 1) * 128],
                                                                           rhs=w2s[:, f, n * 512:(n + 1) * 512], start=(f == 0), stop=(f == NF - 1))
                                   for f in range(NF)],
                            reads=gTr + ["w2s"], writes=[("y", ys)])
                    tf = sl[ys]
                    c.op("dve", lambda e, n=n, ys=ys, tf=tf: e.tensor_tensor(out=tf[:, 0:512], in0=psm[:, 4 + ys, :], in1=Gb[:, n * 512:(n + 1) * 512],
                                                                             op=ALU.mult),
                         reads=[("y", ys), "Gb"], writes=[("sl", ys)])
                    c.op("dve", lambda e, j=j, n=n, tf=tf: e.tensor_tensor(out=X[:, j, n * 512:(n + 1) * 512], in0=tf[:, 0:512],
                                                                           in1=X[:, j, n * 512:(n + 1) * 512], op=ALU.add),
                         reads=[("sl", ys), hx], writes=[hx])
            c.dma("sp", [(dst.rearrange("(j p) d -> p j d", p=128), X[:, 0:nsub, :])], reads=[hx], key=("st", tag))
        c.barrier()
        p.release()

    def phaseB():
        p = Pool(nc)
        w_in_s = p.sb("w_in_s", [128, 8, 1440], BF16)
        w_out_s = p.sb("w_out_s", [128, 8, D], BF16)
        w_uq_s = p.sb("w_uq_s", [128, 2, 768], BF16)
        w_ukv_s = p.sb("w_ukv_s", [128, 1024], BF16)
        wsT_s = p.sb("wsT_s", [128, 8, 128], BF16)
        kva_bc = p.sb("kva_bc", [128, 128], F32)
        qa_bc = p.sb("qa_bc", [128, 256], F32)
        kn_bc = p.sb("kn_bc", [128, 96], F32)
        qn_bc = p.sb("qn_bc", [128, 96], F32)
        vn_bc = p.sb("vn_bc", [128, 512], F32)
        bs_t = p.sb("bs_t", [128, 8], F32)
        cos_t = p.sb("cos_t", [128, 16, 32], F32)
        ssin_t = p.sb("ssin_t", [128, 16, 32], F32)
        KT = p.sb("KT", [96, 8, S + CL], BF16)
        VA = p.sb("VA", [128, 18, 8, 65], BF16)
        A2b = p.sb("A2b", [128, D], F32)
        B2b = p.sb("B2b", [128, D], F32)
        G2b = p.sb("G2b", [128, D], F32)
        psB = p.ps("psB", [128, 8, 512], F32)
        tpB = psB[:, 7, :].bitcast(BF16)

        c.dma("pool", [(w_in_s[:, k, :], w_in_d.rearrange("(k p) n -> p k n", p=128)[:, k, :]) for k in range(8)], writes=["w_in_s"])
        c.dma("pool", [(w_out_s[:, k, :], w_out_d.rearrange("(k p) n -> p k n", p=128)[:, k, :]) for k in range(8)], writes=["w_out_s"])
        c.dma("pool", [(w_uq_s[:], w_uq_d.rearrange("(k p) n -> p k n", p=128))], writes=["w_uq_s"])
        c.dma("pool", [(w_ukv_s[:], w_ukv_d)], writes=["w_ukv_s"])
        c.dma("pool", [(wsT_s[:], wsT_d)], writes=["wsT_s"])
        c.dma("sp", [(kva_bc[:], kva_d[0:1, :].partition_broadcast(128))], writes=["kva_bc"])
        c.dma("sp", [(qa_bc[:], qa_d[0:1, :].partition_broadcast(128))], writes=["qa_bc"])
        c.dma("sp", [(kn_bc[:], kn_d[0:1, :].partition_broadcast(128))], writes=["kn_bc"])
        c.dma("sp", [(qn_bc[:], qn_d[0:1, :].partition_broadcast(128))], writes=["qn_bc"])
        c.dma("sp", [(vn_bc[:], vn_d[0:1, :].partition_broadcast(128))], writes=["vn_bc"])
        c.dma("sp", [(bs_t[:], bsT_d)], writes=["bs_t"])
        c.dma("sp", [(cos_t[:], cos_d)], writes=["cos_t"])
        c.dma("sp", [(ssin_t[:], ssin_d)], writes=["ssin_t"])
        c.op("dve", lambda e: e.tensor_scalar(out=qn_bc[:], in0=qn_bc[:], scalar1=96.0 ** -0.5, scalar2=None, op0=ALU.mult),
             reads=["qn_bc"], writes=["qn_bc"])
        c.op("dve", lambda e: e.memset(VA[:].rearrange("p a b c -> p (a b c)"), 1.0), writes=["VA"])

        def load_mods(row, with_gate):
            c.dma("sp", [(A2b[:], modtab[row:row + 1, 3 * D:4 * D].partition_broadcast(128))], reads=["modtab"], writes=["A2b"])
            c.dma("sp", [(B2b[:], modtab[row:row + 1, 4 * D:5 * D].partition_broadcast(128))], reads=["modtab"], writes=["B2b"])
            if with_gate:
                c.dma("sp", [(G2b[:], modtab[row:row + 1, 5 * D:6 * D].partition_broadcast(128))], reads=["modtab"], writes=["G2b"])

        def rope(pb, src, dst_fn, H, j, hsrc, hdst, tg):
            r1 = pb["r1"][:, 0:H, :]
            r2 = pb["r2"][:, 0:H, :]
            cosb = cos_t[:, j, :].unsqueeze(1).to_broadcast([128, H, 32]) if H > 1 else cos_t[:, j, :].unsqueeze(1)
            c.op("dve", lambda e: e.tensor_tensor(out=r1, in0=src, in1=cosb, op=ALU.mult), reads=[hsrc, "cos_t"], writes=["r1"])
            sv = src.rearrange("p h (a s f) -> p h a s f", a=2, s=2, f=8)
            r2v = r2.rearrange("p h (a s f) -> p h a s f", a=2, s=2, f=8)
            ssv = ssin_t[:, j, :].rearrange("p (a s f) -> p a s f", a=2, s=2, f=8)
            for a in range(2):
                for sdst in range(2):
                    sb_ = ssv[:, a, sdst, :].unsqueeze(1)
                    if H > 1:
                        sb_ = sb_.to_broadcast([128, H, 8])
                    c.op("dve", lambda e, a=a, sdst=sdst, sb_=sb_: e.tensor_tensor(out=r2v[:, :, a, sdst, :], in0=sv[:, :, a, 1 - sdst, :],
                                                                                   in1=sb_, op=ALU.mult),
                         reads=[hsrc, "ssin_t"], writes=["r2"])
            c.op("dve", lambda e: e.tensor_tensor(out=dst_fn(), in0=r1, in1=r2, op=ALU.add), reads=["r1", "r2"], writes=[hdst])

        for b in range(nb):
            p1 = Pool(nc)
            pb = {}
            xs = [p1.sb("xs%d" % i, [128, D], F32) for i in range(2)]
            junk = p1.sb("junkB", [128, D], BF16)
            tmpf = p1.sb("tmpfB", [128, D], F32)
            hb = p1.sb("hbB", [128, D], BF16)
            hTs = p1.sb("hTs", [128, 8, 128], BF16)
            st1 = p1.sb("st1", [128, 4], F32)
            ln1 = p1.sb("ln1", [128, 4], F32)
            rs1 = p1.sb("rs1", [128, 4], F32)
            ckvn = p1.sb("ckvn", [128, 128], BF16)
            ckvT = p1.sb("ckvT", [128, 128], BF16)
            kvsq = p1.sb("kvsq", [128, 1024], F32)
            ssk = p1.sb("ssk", [128, 8], F32)
            lnk = p1.sb("lnk", [128, 8], F32)
            rsk = p1.sb("rsk", [128, 8], F32)
            kt1 = p1.sb("kt1", [128, 8, 64], F32)
            Ktm = p1.sb("Ktm", [128, 8, 96], BF16)
            kp = p1.sb("kp", [128, 1, 32], F32)
            kpr = p1.sb("kpr", [128, 1, 32], F32)
            pb["r1"] = p1.sb("r1", [128, 8, 32], F32)
            pb["r2"] = p1.sb("r2", [128, 8, 32], F32)
            cqn = p1.sb("cqn", [128, 256], BF16)
            cqT = p1.sb("cqT", [128, 2, 128], BF16)
            qsq = p1.sb("qsq", [128, 768], F32)
            ssq = p1.sb("ssq", [128, 8], F32)
            lnq = p1.sb("lnq", [128, 8], F32)
            rsq = p1.sb("rsq", [128, 8], F32)
            qt = p1.sb("qt", [128, 8, 96], F32)
            qpe = p1.sb("qpe", [128, 8, 32], F32)
            Qtm = p1.sb("Qtm", [128, 8, 96], BF16)
            QTst = p1.sb("QTst", [96, 8, 128], BF16)
            g1 = p1.sb("g1", [128, 1024], F32)
            g2 = p1.sb("g2", [128, 1024], F32)
            gl = p1.sb("gl", [128, 1024], F32)
            ssv = p1.sb("ssv", [128, 8], F32)
            lnvv = p1.sb("lnvv", [128, 8], F32)
            rsv = p1.sb("rsv", [128, 8], F32)
            vn = p1.sb("vn", [128, 512], BF16)
            sgtm = p1.sb("sgtm", [128, 512], BF16)
            sgTst = p1.sb("sgTst", [128, 4, 128], BF16)

            order = [(True, 0), (True, 1)] + [(False, j) for j in range(16)]
            cur = None
            for si, (is_ctx, j) in enumerate(order):
                row = nb if is_ctx else b
                if cur != row:
                    cur = row
                    load_mods(row, not is_ctx)
                kt = 16 + j if is_ctx else j
                r0 = (S + j * 128) if is_ctx else j * 128
                X = xs[si % 2]
                hx = ("xs", si % 2)
                c.dma("sp", [(X[:], x1s[b, r0:r0 + 128, :])], reads=["x1s"], writes=[hx])
                c.op("dve", lambda e: e.memset(st1[:], 0.0), writes=["st1"])
                c.op("act", lambda e: e.activation(out=junk[:], in_=X[:], func=AF.Square, accum_out=st1[:, 0:1]),
                     reads=[hx, "st1"], writes=["junk", "st1"])
                rstd_from_ss(st1[:, 0:1], rs1[:, 0:1], ln1[:, 0:1], D, "st1", "ln1", "rs1")
                c.op("dve", lambda e: e.scalar_tensor_tensor(out=tmpf[:], in0=X[:], scalar=rs1[:, 0:1], in1=A2b[:], op0=ALU.mult, op1=ALU.mult),
                     reads=[hx, "rs1", "A2b"], writes=["tmpf"])
                c.op("dve", lambda e: e.tensor_tensor(out=hb[:], in0=tmpf[:], in1=B2b[:], op=ALU.add), reads=["tmpf", "B2b"], writes=["hb"])
                c.group("pe", [lambda e, k=k: e.transpose(out=tpB[:, k * 128:(k + 1) * 128], in_=hb[:, k * 128:(k + 1) * 128], identity=ident[:])
                               for k in range(8)], reads=["hb", "ident"], writes=["tpB"])
                c.op("act", lambda e: e.copy(out=hTs[:], in_=tpB.rearrange("p (k t) -> p k t", t=128)), reads=["tpB"], writes=["hTs"])
                n0 = 160 if is_ctx else 416
                c.group("pe", [lambda e, k=k: e.matmul(out=psB[:, 0, 0:n0], lhsT=hTs[:, k, :], rhs=w_in_s[:, k, 0:n0],
                                                       start=(k == 0), stop=(k == 7)) for k in range(8)],
                        reads=["hTs", "w_in_s"], writes=["P0"])
                if not is_ctx:
                    for (bank, c0, hh) in ((1, 416, "P1"), (2, 928, "P2")):
                        c.group("pe", [lambda e, k=k, bank=bank, c0=c0: e.matmul(out=psB[:, bank, :], lhsT=hTs[:, k, :], rhs=w_in_s[:, k, c0:c0 + 512],
                                                                                 start=(k == 0), stop=(k == 7)) for k in range(8)],
                                reads=["hTs", "w_in_s"], writes=[hh])
                c.op("act", lambda e: e.activation(out=junk[:, 0:128], in_=psB[:, 0, 0:128], func=AF.Square, accum_out=st1[:, 1:2]),
                     reads=["P0", "st1"], writes=["junk", "st1"])
                c.op("act", lambda e: e.activation(out=junk[:, 128:160], in_=psB[:, 0, 128:160], func=AF.Square, accum_out=st1[:, 3:4]),
                     reads=["P0", "st1"], writes=["junk", "st1"])
                rstd_from_ss(st1[:, 1:2], rs1[:, 1:2], ln1[:, 1:2], 128, "st1", "ln1", "rs1")
                if not is_ctx:
                    c.op("act", lambda e: e.activation(out=junk[:, 160:416], in_=psB[:, 0, 160:416], func=AF.Square, accum_out=st1[:, 2:3]),
                         reads=["P0", "st1"], writes=["junk", "st1"])
                    rstd_from_ss(st1[:, 2:3], rs1[:, 2:3], ln1[:, 2:3], 256, "st1", "ln1", "rs1")
                c.op("dve", lambda e: e.scalar_tensor_tensor(out=ckvn[:], in0=psB[:, 0, 0:128], scalar=rs1[:, 1:2], in1=kva_bc[:],
                                                             op0=ALU.mult, op1=ALU.mult),
                     reads=["P0", "rs1", "kva_bc"], writes=["ckvn"])
                c.group("pe", [lambda e: e.transpose(out=tpB[:, 0:128], in_=ckvn[:], identity=ident[:])], reads=["ckvn", "ident"], writes=["tpB"])
                c.op("act", lambda e: e.copy(out=ckvT[:], in_=tpB[:, 0:128]), reads=["tpB"], writes=["ckvT"])
                for hf in range(2):
                    c.group("pe", [lambda e, hf=hf: e.matmul(out=psB[:, 3 + hf, :], lhsT=ckvT[:], rhs=w_ukv_s[:, hf * 512:(hf + 1) * 512],
                                                             start=True, stop=True)],
                            reads=["ckvT", "w_ukv_s"], writes=[("KV", hf)])
                kvv = psB[:, 3:5, :].rearrange("p a (h c) -> p (a h) c", c=128)
                c.op("act", lambda e: e.activation(out=kvsq[:].rearrange("p (a n) -> p a n", a=2), in_=psB[:, 3:5, :], func=AF.Square),
                     reads=[("KV", 0), ("KV", 1)], writes=["kvsq"])
                c.op("dve", lambda e: e.tensor_reduce(out=ssk[:], in_=kvsq[:].rearrange("p (h c) -> p h c", c=128)[:, :, 0:64], axis=AX.X, op=ALU.add),
                     reads=["kvsq"], writes=["ssk"])
                c.op("dve", lambda e: e.tensor_scalar(out=ssk[:], in0=ssk[:], scalar1=st1[:, 3:4], scalar2=None, op0=ALU.add),
                     reads=["ssk", "st1"], writes=["ssk"])
                rstd_from_ss(ssk[:], rsk[:], lnk[:], 96, "ssk", "lnk", "rsk")
                c.op("dve", lambda e: e.tensor_tensor(out=kt1[:], in0=kvv[:, :, 0:64], in1=rsk[:].unsqueeze(2).to_broadcast([128, 8, 64]), op=ALU.mult),
                     reads=[("KV", 0), ("KV", 1), "rsk"], writes=["kt1"])
                c.op("dve", lambda e: e.tensor_tensor(out=Ktm[:, :, 0:64], in0=kt1[:], in1=kn_bc[:, 0:64].unsqueeze(1).to_broadcast([128, 8, 64]),
                                                      op=ALU.mult),
                     reads=["kt1", "kn_bc"], writes=["Ktm"])
                c.op("dve", lambda e: e.tensor_tensor(out=kp[:, 0, :], in0=psB[:, 0, 128:160], in1=kn_bc[:, 64:96], op=ALU.mult),
                     reads=["P0", "kn_bc"], writes=["kp"])
                if is_ctx:
                    kfin, hk = kp, "kp"
                else:
                    rope(pb, kp[:], lambda: kpr[:], 1, j, "kp", "kpr", "k")
                    kfin, hk = kpr, "kpr"
                c.op("dve", lambda e, kfin=kfin: e.tensor_tensor(out=Ktm[:, :, 64:96], in0=kfin[:, 0, :].unsqueeze(1).to_broadcast([128, 8, 32]),
                                                                 in1=rsk[:].unsqueeze(2).to_broadcast([128, 8, 32]), op=ALU.mult),
                     reads=[hk, "rsk"], writes=["Ktm"])
                c.group("pe", [lambda e, h=h: e.transpose(out=tpB[0:96, h * 128:(h + 1) * 128], in_=Ktm[:, h, :], identity=ident[:])
                               for h in range(8)], reads=["Ktm", "ident"], writes=["tpB"])
                c.op("act", lambda e, kt=kt: e.copy(out=KT[:, :, kt * 128:(kt + 1) * 128], in_=tpB[0:96, :].rearrange("p (h t) -> p h t", t=128)),
                     reads=["tpB"], writes=["KT"])
                c.op("act", lambda e, kt=kt: e.copy(out=VA[:, kt, :, 0:64], in_=kvv[:, :, 64:128]), reads=[("KV", 0), ("KV", 1)], writes=["VA"])
                if is_ctx:
                    continue
                c.op("dve", lambda e: e.scalar_tensor_tensor(out=cqn[:], in0=psB[:, 0, 160:416], scalar=rs1[:, 2:3], in1=qa_bc[:],
                                                             op0=ALU.mult, op1=ALU.mult),
                     reads=["P0", "rs1", "qa_bc"], writes=["cqn"])
                c.group("pe", [lambda e, k=k: e.transpose(out=tpB[:, k * 128:(k + 1) * 128], in_=cqn[:, k * 128:(k + 1) * 128], identity=ident[:])
                               for k in range(2)], reads=["cqn", "ident"], writes=["tpB"])
                c.op("act", lambda e: e.copy(out=cqT[:], in_=tpB[:, 0:256].rearrange("p (k t) -> p k t", t=128)), reads=["tpB"], writes=["cqT"])
                for hf in range(2):
                    c.group("pe", [lambda e, k=k, hf=hf: e.matmul(out=psB[:, 5 + hf, 0:384], lhsT=cqT[:, k, :], rhs=w_uq_s[:, k, hf * 384:(hf + 1) * 384],
                                                                  start=(k == 0), stop=(k == 1)) for k in range(2)],
                            reads=["cqT", "w_uq_s"], writes=[("Q", hf)])
                for hf in range(2):
                    c.op("act", lambda e, hf=hf: e.activation(out=qsq[:, hf * 384:(hf + 1) * 384], in_=psB[:, 5 + hf, 0:384], func=AF.Square),
                         reads=[("Q", hf)], writes=["qsq"])
                c.op("dve", lambda e: e.tensor_reduce(out=ssq[:], in_=qsq[:].rearrange("p (h c) -> p h c", c=96), axis=AX.X, op=ALU.add),
                     reads=["qsq"], writes=["ssq"])
                rstd_from_ss(ssq[:], rsq[:], lnq[:], 96, "ssq", "lnq", "rsq")
                for hf in range(2):
                    c.op("dve", lambda e, hf=hf: e.tensor_tensor(out=qt[:, hf * 4:(hf + 1) * 4, :],
                                                                 in0=psB[:, 5 + hf, 0:384].rearrange("p (h c) -> p h c", c=96),
                                                                 in1=rsq[:, hf * 4:(hf + 1) * 4].unsqueeze(2).to_broadcast([128, 4, 96]), op=ALU.mult),
                         reads=[("Q", hf), "rsq"], writes=["qt"])
                c.op("dve", lambda e: e.tensor_tensor(out=Qtm[:, :, 0:64], in0=qt[:, :, 0:64], in1=qn_bc[:, 0:64].unsqueeze(1).to_broadcast([128, 8, 64]),
                                                      op=ALU.mult),
                     reads=["qt", "qn_bc"], writes=["Qtm"])
                c.op("dve", lambda e: e.tensor_tensor(out=qpe[:], in0=qt[:, :, 64:96], in1=qn_bc[:, 64:96].unsqueeze(1).to_broadcast([128, 8, 32]),
                                                      op=ALU.mult),
                     reads=["qt", "qn_bc"], writes=["qpe"])
                rope(pb, qpe[:], lambda: Qtm[:, :, 64:96], 8, j, "qpe", "Qtm", "q")
                c.group("pe", [lambda e, h=h: e.transpose(out=tpB[0:96, h * 128:(h + 1) * 128], in_=Qtm[:, h, :], identity=ident[:])
                               for h in range(8)], reads=["Qtm", "ident"], writes=["tpB"])
                c.op("act", lambda e: e.copy(out=QTst[:], in_=tpB[0:96, :].rearrange("p (h t) -> p h t", t=128)), reads=["tpB"], writes=["QTst"])
                c.dma("sp", [(qts[b, :, :, j * 128:(j + 1) * 128], QTst[:])], reads=["QTst"], writes=["qts"], key="qts_st")
                uv = psB[:, 1:3, :]
                v2 = lambda t: t[:].rearrange("p (a n) -> p a n", a=2)
                c.op("act", lambda e: e.activation(out=v2(g1), in_=uv, func=AF.Square), reads=["P1", "P2"], writes=["g1"])
                c.op("dve", lambda e: e.tensor_scalar(out=g1[:], in0=g1[:], scalar1=0.044715, scalar2=1.0, op0=ALU.mult, op1=ALU.add),
                     reads=["g1"], writes=["g1"])
                c.op("dve", lambda e: e.tensor_tensor(out=v2(g2), in0=v2(g1), in1=uv, op=ALU.mult), reads=["g1", "P1", "P2"], writes=["g2"])
                c.op("act", lambda e: e.activation(out=g1[:], in_=g2[:], func=AF.Exp, scale=-1.5957691216057308), reads=["g2"], writes=["g1"])
                c.op("act", lambda e: e.activation(out=g2[:], in_=g1[:], func=AF.Ln, scale=1.0, bias=oneb[:, 0:1]), reads=["g1", "oneb"], writes=["g2"])
                c.op("act", lambda e: e.activation(out=g1[:], in_=g2[:], func=AF.Exp, scale=-1.0), reads=["g2"], writes=["g1"])
                c.op("dve", lambda e: e.tensor_tensor(out=v2(gl), in0=v2(g1), in1=uv, op=ALU.mult), reads=["g1", "P1", "P2"], writes=["gl"])
                c.op("act", lambda e: e.activation(out=g1[:, 0:512], in_=gl[:, 512:1024], func=AF.Square), reads=["gl"], writes=["g1"])
                c.op("dve", lambda e: e.tensor_reduce(out=ssv[:], in_=g1[:, 0:512].rearrange("p (g c) -> p g c", c=64), axis=AX.X, op=ALU.add),
                     reads=["g1"], writes=["ssv"])
                rstd_from_ss(ssv[:], rsv[:], lnvv[:], 64, "ssv", "lnvv", "rsv")
                c.op("dve", lambda e: e.tensor_tensor(out=g2[:, 0:512].rearrange("p (g c) -> p g c", c=64),
                                                      in0=gl[:, 512:1024].rearrange("p (g c) -> p g c", c=64),
                                                      in1=rsv[:].unsqueeze(2).to_broadcast([128, 8, 64]), op=ALU.mult),
                     reads=["gl", "rsv"], writes=["g2"])
                c.op("dve", lambda e: e.tensor_tensor(out=vn[:], in0=g2[:, 0:512], in1=vn_bc[:], op=ALU.mult), reads=["g2", "vn_bc"], writes=["vn"])
                c.group("pe", [lambda e, g=g: e.matmul(out=psB[:, 3, g * 64:(g + 1) * 64], lhsT=wsT_s[:, g, :], rhs=vn[:, g * 64:(g + 1) * 64],
                                                       start=True, stop=True) for g in range(8)],
                        reads=["vn", "wsT_s"], writes=[("KV", 0)])
                c.op("dve", lambda e: e.tensor_tensor(out=g1[:, 0:512].rearrange("p (g c) -> p g c", c=64),
                                                      in0=psB[:, 3, :].rearrange("p (g c) -> p g c", c=64),
                                                      in1=bs_t[:].unsqueeze(2).to_broadcast([128, 8, 64]), op=ALU.add),
                     reads=[("KV", 0), "bs_t"], writes=["g1"])
                c.op("dve", lambda e: e.tensor_tensor(out=sgtm[:], in0=g1[:, 0:512], in1=gl[:, 0:512], op=ALU.mult), reads=["g1", "gl"], writes=["sgtm"])
                c.group("pe", [lambda e, k=k: e.transpose(out=tpB[:, k * 128:(k + 1) * 128], in_=sgtm[:, k * 128:(k + 1) * 128], identity=ident[:])
                               for k in range(4)], reads=["sgtm", "ident"], writes=["tpB"])
                c.op("act", lambda e: e.copy(out=sgTst[:], in_=tpB[:, 0:512].rearrange("p (k t) -> p k t", t=128)), reads=["tpB"], writes=["sgTst"])
                c.dma("sp", [(sgs[b, :, :, j * 128:(j + 1) * 128], sgTst[:])], reads=["sgTst"], writes=["sgs"], key="sgs_st")
            c.barrier()
            p1.release()

            p2 = Pool(nc)
            QT = [p2.sb("QT%d" % i, [96, 8, 512], BF16) for i in range(2)]
            sgT = [p2.sb("sgT%d" % i, [128, 4, 512], BF16) for i in range(2)]
            NPT = 6
            PT = [p2.sb("PT%d" % i, [128, 512], BF16) for i in range(NPT)]
            rden = p2.sb("rden", [128, 4], F32)
            attn = p2.sb("attn", [128, 4, 512], BF16)
            attnT = p2.sb("attnT", [128, 4, 512], BF16)
            xr = [p2.sb("xr%d" % i, [128, D], F32) for i in range(2)]
            tmp2 = [p2.sb("tmp2%d" % i, [128, 512], F32) for i in range(2)]
            for t in range(4):
                s = t % 2
                c.dma("sp", [(QT[s][:], qts[b, :, :, t * 512:(t + 1) * 512])], reads=["qts"], writes=[("QT", s)])
                c.dma("sp", [(sgT[s][:], sgs[b, :, :, t * 512:(t + 1) * 512])], reads=["sgs"], writes=[("sgT", s)])
                cnt = 0
                for h in range(8):
                    def S_mm(kt):
                        c.group("pe", [lambda e: e.matmul(out=psB[:, kt % 2, :], lhsT=KT[:, h, kt * 128:(kt + 1) * 128], rhs=QT[s][:, h, :],
                                                          start=True, stop=True)],
                                reads=["KT", ("QT", s)], writes=[("ST", kt % 2)])
                    S_mm(0)
                    for kt in range(18):
                        if kt + 1 < 18:
                            S_mm(kt + 1)
                        sl_ = cnt % NPT
                        cnt += 1
                        c.op("act", lambda e, kt=kt, sl_=sl_: e.activation(out=PT[sl_][:], in_=psB[:, kt % 2, :], func=AF.Exp),
                             reads=[("ST", kt % 2)], writes=[("PT", sl_)])
                        c.group("pe", [lambda e, jj=jj, kt=kt, sl_=sl_: e.matmul(out=psB[:, 2 + jj, 0:65], lhsT=PT[sl_][:, jj * 128:(jj + 1) * 128],
                                                                                 rhs=VA[:, kt, h, :], start=(kt == 0), stop=(kt == 17))
                                       for jj in range(4)],
                                reads=[("PT", sl_), "VA"], writes=["O"])
                    c.op("dve", lambda e: e.reciprocal(out=rden[:].unsqueeze(2), in_=psB[:, 2:6, 64:65]), reads=["O"], writes=["rden"])
                    c.op("dve", lambda e, h=h: e.tensor_tensor(out=attn[:, :, h * 64:(h + 1) * 64], in0=psB[:, 2:6, 0:64],
                                                               in1=rden[:].unsqueeze(2).to_broadcast([128, 4, 64]), op=ALU.mult),
                         reads=["O", "rden"], writes=["attn"])
                for jj in range(4):
                    c.group("pe", [lambda e, k=k, jj=jj: e.transpose(out=tpB[:, k * 128:(k + 1) * 128], in_=attn[:, jj, k * 128:(k + 1) * 128],
                                                                     identity=ident[:]) for k in range(4)],
                            reads=["attn", "ident"], writes=["tpB"])
                    c.op("act", lambda e, jj=jj: e.copy(out=attnT[:, :, jj * 128:(jj + 1) * 128], in_=tpB[:, 0:512].rearrange("p (k t) -> p k t", t=128)),
                         reads=["tpB"], writes=["attnT"])
                for jj in range(4):
                    r0 = t * 512 + jj * 128
                    XR = xr[jj % 2]
                    hxr = ("xr", jj % 2)
                    c.dma("sp", [(XR[:], x1s[b, r0:r0 + 128, :])], reads=["x1s"], writes=[hxr])
                    for n in range(2):
                        fns = []
                        for k in range(8):
                            src = attnT if k < 4 else sgT[s]
                            fns.append(lambda e, k=k, src=src, n=n, jj=jj: e.matmul(out=psB[:, 6, :], lhsT=src[:, k % 4, jj * 128:(jj + 1) * 128],
                                                                                   rhs=w_out_s[:, k, n * 512:(n + 1) * 512], start=(k == 0), stop=(k == 7)))
                        c.group("pe", fns, reads=["attnT", ("sgT", s), "w_out_s"], writes=["Y"])
                        c.op("dve", lambda e, n=n: e.tensor_tensor(out=tmp2[n][:], in0=psB[:, 6, :], in1=G2b[:, n * 512:(n + 1) * 512], op=ALU.mult),
                             reads=["Y", "G2b"], writes=[("tmp2", n)])
                        c.op("dve", lambda e, n=n, XR=XR: e.tensor_tensor(out=XR[:, n * 512:(n + 1) * 512], in0=tmp2[n][:], in1=XR[:, n * 512:(n + 1) * 512],
                                                                          op=ALU.add),
                             reads=[("tmp2", n), hxr], writes=[hxr])
                    c.dma("sp", [(x2s[b, r0:r0 + 128, :], XR[:])], reads=[hxr], writes=["x2s"], key=("x2st", jj % 2))
            c.barrier()
            p2.release()
        p.release()

    phase0()
    if stop_after < 1:
        c.final_wait()
        return nc
    tilesA = []
    for b in range(nb):
        for t in range(4):
            tilesA.append((x_d[b, t * 512:(t + 1) * 512, :], x1s[b, t * 512:(t + 1) * 512, :], 4, b))
    for b in range(nb):
        tilesA.append((ctx_d[b, :, :], x1s[b, S:S + CL, :], 2, nb))
    ffn_phase("A", fw_d[1], (0, 1, 2), tilesA)
    if stop_after < 2:
        c.final_wait()
        return nc
    phaseB()
    if stop_after < 3:
        c.final_wait()
        return nc
    tilesC = []
    for b in range(nb):
        for t in range(4):
            tilesC.append((x2s[b, t * 512:(t + 1) * 512, :], out_d[b, t * 512:(t + 1) * 512, :], 4, b))
    ffn_phase("C", fw_d[2], (6, 7, 8), tilesC)
    c.final_wait()
    return nc


def _rope_tables():
    f32 = np.float32
    rows = np.repeat(np.arange(S // GRID_W, dtype=f32), GRID_W)
    cols = np.tile(np.arange(GRID_W, dtype=f32), S // GRID_W)
    inv = (np.float32(10000.0) ** (-np.arange(0, 16, 2, dtype=f32) / np.float32(16))).astype(f32)
    ang_r = rows[:, None] * inv
    ang_c = cols[:, None] * inv
    ang = np.concatenate([ang_r, ang_r, ang_c, ang_c], axis=-1).astype(f32)
    cos = np.cos(ang).astype(f32)
    sin = np.sin(ang).astype(f32)
    sign = np.tile(np.concatenate([-np.ones(8, f32), np.ones(8, f32)]), 2)
    ssin = sin * sign[None, :]
    to_t = lambda a: np.ascontiguousarray(a.reshape(16, 128, 32).transpose(1, 0, 2))
    return to_t(cos), to_t(ssin)


def make_in_maps(inputs, nb, ncores):
    f = lambda a: np.ascontiguousarray(np.asarray(a, dtype=np.float32))
    x = f(inputs["x"]); cvec = f(inputs["c"]); ctxa = f(inputs["ctx"]); c_ctx = f(inputs["c_ctx"])
    cos_t, ssin_t = _rope_tables()
    shared = {
        "w_ada": f(inputs["w_ada"][0]), "b_ada": f(inputs["b_ada"][0]).reshape(1, -1),
        "normw": f(np.stack([inputs["norm1_w"][0], inputs["norm2_w"][0], inputs["norm3_w"][0]])),
        "ffn1_w1": f(inputs["ffn1_w1"][0]), "ffn1_w3": f(inputs["ffn1_w3"][0]), "ffn1_w2": f(inputs["ffn1_w2"][0]),
        "ffn2_w1": f(inputs["ffn2_w1"][0]), "ffn2_w3": f(inputs["ffn2_w3"][0]), "ffn2_w2": f(inputs["ffn2_w2"][0]),
        "w_in": f(inputs["w_in"][0]), "q_a_norm_w": f(inputs["q_a_norm_w"][0]).reshape(1, -1), "w_uq": f(inputs["w_uq"][0]),
        "kv_a_norm_w": f(inputs["kv_a_norm_w"][0]).reshape(1, -1), "w_ukv": f(inputs["w_ukv"][0]),
        "q_norm_w": f(inputs["q_norm_w"][0]).reshape(1, -1), "k_norm_w": f(inputs["k_norm_w"][0]).reshape(1, -1),
        "v_norm_w": f(inputs["v_norm_w"][0]).reshape(1, -1),
        "w_sT": f(np.transpose(np.asarray(inputs["w_s"][0]), (2, 0, 1))),
        "b_sT": f(np.transpose(np.asarray(inputs["b_s"][0]), (1, 0))),
        "w_out": f(inputs["w_out"][0]),
        "cos_t": cos_t, "ssin_t": ssin_t,
        "ident": np.eye(128, dtype=np.float32).astype(ml_dtypes.bfloat16),
    }
    maps = []
    for i in range(ncores):
        cs = np.concatenate([cvec[i * nb:(i + 1) * nb], c_ctx[None, :]], axis=0)
        cT = np.ascontiguousarray(cs.T.reshape(8, 128, nb + 1).transpose(1, 0, 2))
        m = dict(shared)
        m["x"] = np.ascontiguousarray(x[i * nb:(i + 1) * nb])
        m["ctx"] = np.ascontiguousarray(ctxa[i * nb:(i + 1) * nb])
        m["cT"] = cT
        maps.append(m)
    return maps


_NC_CACHE = {}


def kernel(**inputs):
    B = inputs["x"].shape[0]
    nb = B // NCORES
    if nb not in _NC_CACHE:
        _NC_CACHE[nb] = build(nb)
    nc = _NC_CACHE[nb]
    in_maps = make_in_maps(inputs, nb, NCORES)
    res = run_bass_kernel_spmd(nc, in_maps, core_ids=list(range(NCORES)))
    return np.concatenate([np.asarray(r["out"], dtype=np.float32) for r in res.results], axis=0)
```
